# Optimizing a Trainium2 kernel written in Bass

```python
import math
import jax
import jax.numpy as jnp
from jax import lax
import numpy as np

D_MODEL = 1024
BATCH = 1
SEQ = 16384
DEPTH = 2
DEC_BATCH = 4
DEC_SEQ = 4096
PAST_LEN = 128

N_META = 16
BLOCK = 128
WINDOW = 128
ROPE_THETA = 10000.0
EPS = 1e-6
NEG_INF = -1e30

DA_HEADS = 4
DA_HEAD_DIM = 64
DA_V_DIM = 2 * DA_HEAD_DIM
MLA_HEADS = 4
MLA_Q_RANK = 384
MLA_KV_RANK = 256
MLA_NOPE = 128
MLA_ROPE = 64
MLA_QK_DIM = MLA_NOPE + MLA_ROPE
MLA_V = 128
GQA_HEADS = 8
GQA_KV_HEADS = 2
GQA_GROUP = GQA_HEADS // GQA_KV_HEADS
GQA_HEAD_DIM = 64

N_BRANCH = 3
BRANCH_WIDTH = 512
D_FF = 4 * D_MODEL

DA_Q_COLS = DA_HEADS * 2 * DA_HEAD_DIM
DA_K_COLS = DA_HEADS * 2 * DA_HEAD_DIM
DA_V_COLS = DA_HEADS * DA_V_DIM
MLA_CQ_COLS = MLA_Q_RANK
MLA_CKV_COLS = MLA_KV_RANK
MLA_KR_COLS = MLA_ROPE
GQA_Q_COLS = GQA_HEADS * GQA_HEAD_DIM
GQA_K_COLS = GQA_KV_HEADS * GQA_HEAD_DIM
GQA_V_COLS = GQA_KV_HEADS * GQA_HEAD_DIM
GATE_COLS = N_BRANCH * D_MODEL
IN_SIZES = (DA_Q_COLS, DA_K_COLS, DA_V_COLS, MLA_CQ_COLS, MLA_CKV_COLS, MLA_KR_COLS,
            GQA_Q_COLS, GQA_K_COLS, GQA_V_COLS, GATE_COLS)
IN_SPLITS = tuple(sum(IN_SIZES[:i + 1]) for i in range(len(IN_SIZES) - 1))
IN_WIDTH = sum(IN_SIZES)

kernel_name = "hybrid_diff_mla_swa_gated_encoder"


def rms_norm(x, g):
    xf = x.astype(jnp.float32)
    y = xf * lax.rsqrt(jnp.mean(xf * xf, axis=-1, keepdims=True) + EPS)
    return (y * g.astype(jnp.float32)).astype(x.dtype)


def rope_tables(length, dim):
    inv_freq = 1.0 / (ROPE_THETA ** (jnp.arange(0, dim, 2, dtype=jnp.float32) / dim))
    ang = jnp.arange(length, dtype=jnp.float32)[:, None] * inv_freq[None, :]
    return jnp.cos(ang), jnp.sin(ang)


def apply_rope(x, cos, sin):
    xf = x.astype(jnp.float32)
    x1, x2 = jnp.split(xf, 2, axis=-1)
    c = cos[None, :, None, :]
    s = sin[None, :, None, :]
    return jnp.concatenate([x1 * c - x2 * s, x1 * s + x2 * c], axis=-1).astype(x.dtype)


def sweep_query_blocks(attend, qs):
    B, L = qs[0].shape[:2]
    nb = (L - N_META) // BLOCK
    meta_out = attend(tuple(q[:, :N_META] for q in qs))
    blocks = tuple(jnp.moveaxis(q[:, N_META:].reshape((B, nb, BLOCK) + q.shape[2:]), 1, 0) for q in qs)
    real_out = lax.map(attend, blocks)
    real_out = jnp.moveaxis(real_out, 0, 1).reshape((B, nb * BLOCK) + real_out.shape[3:])
    return jnp.concatenate([meta_out, real_out], axis=1)


def diff_attention(q, k, v, q_norm_g, k_norm_g, lam_q1, lam_k1, lam_q2, lam_k2, subln_g, lam_init):
    B, L, _ = q.shape
    cos, sin = rope_tables(L, DA_HEAD_DIM)
    q = apply_rope(rms_norm(q.reshape(B, L, DA_HEADS * 2, DA_HEAD_DIM), q_norm_g), cos, sin)
    k = apply_rope(rms_norm(k.reshape(B, L, DA_HEADS * 2, DA_HEAD_DIM), k_norm_g), cos, sin)
    q = q.reshape(B, L, DA_HEADS, 2, DA_HEAD_DIM)
    k = k.reshape(B, L, DA_HEADS, 2, DA_HEAD_DIM)
    q1, q2 = q[..., 0, :], q[..., 1, :]
    k1, k2 = k[..., 0, :], k[..., 1, :]
    v = v.reshape(B, L, DA_HEADS, DA_V_DIM)
    lam = (jnp.exp(jnp.sum(lam_q1.astype(jnp.float32) * lam_k1.astype(jnp.float32)))
           - jnp.exp(jnp.sum(lam_q2.astype(jnp.float32) * lam_k2.astype(jnp.float32)))
           + lam_init)
    scale = DA_HEAD_DIM ** -0.5

    def attend(qb):
        qb1, qb2 = qb
        p1 = jax.nn.softmax(jnp.einsum('bqhd,bkhd->bhqk', qb1, k1).astype(jnp.float32) * scale, axis=-1)
        p2 = jax.nn.softmax(jnp.einsum('bqhd,bkhd->bhqk', qb2, k2).astype(jnp.float32) * scale, axis=-1)
        a = (p1 - lam * p2).astype(v.dtype)
        return jnp.einsum('bhqk,bkhd->bqhd', a, v)

    o = sweep_query_blocks(attend, (q1, q2))
    o = rms_norm(o, subln_g) * (1.0 - lam_init)
    return o.reshape(B, L, DA_HEADS * DA_V_DIM)


def mla_attention(c_q, c_kv, k_rope, cq_norm_g, ckv_norm_g, w_uq, w_ukv, q_norm_g, k_norm_g):
    B, L, _ = c_q.shape
    cos, sin = rope_tables(L, MLA_ROPE)
    q = (rms_norm(c_q, cq_norm_g) @ w_uq).reshape(B, L, MLA_HEADS, MLA_QK_DIM)
    kv = (rms_norm(c_kv, ckv_norm_g) @ w_ukv).reshape(B, L, MLA_HEADS, MLA_NOPE + MLA_V)
    k_nope, v = kv[..., :MLA_NOPE], kv[..., MLA_NOPE:]
    k_r = jnp.broadcast_to(k_rope[:, :, None, :], (B, L, MLA_HEADS, MLA_ROPE))
    k = jnp.concatenate([k_nope, k_r], axis=-1)
    q = rms_norm(q, q_norm_g)
    k = rms_norm(k, k_norm_g)
    q = jnp.concatenate([q[..., :MLA_NOPE], apply_rope(q[..., MLA_NOPE:], cos, sin)], axis=-1)
    k = jnp.concatenate([k[..., :MLA_NOPE], apply_rope(k[..., MLA_NOPE:], cos, sin)], axis=-1)
    scale = MLA_QK_DIM ** -0.5

    def attend(qbs):
        (qb,) = qbs
        p = jax.nn.softmax(jnp.einsum('bqhd,bkhd->bhqk', qb, k).astype(jnp.float32) * scale, axis=-1)
        return jnp.einsum('bhqk,bkhd->bqhd', p.astype(v.dtype), v)

    o = sweep_query_blocks(attend, (q,))
    return o.reshape(B, L, MLA_HEADS * MLA_V)


def window_gqa_attention(q, k, v, q_norm_g, k_norm_g, sink):
    B, L, _ = q.shape
    nb = (L - N_META) // BLOCK
    cos, sin = rope_tables(L, GQA_HEAD_DIM)
    q = apply_rope(rms_norm(q.reshape(B, L, GQA_HEADS, GQA_HEAD_DIM), q_norm_g), cos, sin)
    k = apply_rope(rms_norm(k.reshape(B, L, GQA_KV_HEADS, GQA_HEAD_DIM), k_norm_g), cos, sin)
    v = v.reshape(B, L, GQA_KV_HEADS, GQA_HEAD_DIM)
    q = q.reshape(B, L, GQA_KV_HEADS, GQA_GROUP, GQA_HEAD_DIM)
    scale = GQA_HEAD_DIM ** -0.5
    sink_f = sink.astype(jnp.float32).reshape(GQA_KV_HEADS, GQA_GROUP)

    def sink_softmax(s):
        sk = jnp.broadcast_to(sink_f[:, :, None, None], s.shape[:-1] + (1,))
        return jax.nn.softmax(jnp.concatenate([s, sk], axis=-1), axis=-1)[..., :-1]

    nk_meta = N_META + BLOCK
    s_m = jnp.einsum('bqkgd,bskd->bkgqs', q[:, :N_META], k[:, :nk_meta]).astype(jnp.float32) * scale
    qi_m = jnp.arange(N_META)[:, None]
    kj_m = jnp.arange(nk_meta)[None, :]
    s_m = jnp.where(kj_m <= qi_m + WINDOW, s_m, NEG_INF)
    o_meta = jnp.einsum('bkgqs,bskd->bqkgd', sink_softmax(s_m).astype(v.dtype), v[:, :nk_meta])

    k_meta, v_meta = k[:, :N_META], v[:, :N_META]
    qr = q[:, N_META:].reshape(B, nb, BLOCK, GQA_KV_HEADS, GQA_GROUP, GQA_HEAD_DIM)

    def band(t):
        t = t[:, N_META:].reshape(B, nb, BLOCK, GQA_KV_HEADS, GQA_HEAD_DIM)
        tp = jnp.pad(t, ((0, 0), (1, 1), (0, 0), (0, 0), (0, 0)))
        return jnp.concatenate([tp[:, :-2], tp[:, 1:-1], tp[:, 2:]], axis=2)

    kb, vb = band(k), band(v)
    s_band = jnp.einsum('bnqkgd,bnskd->bnkgqs', qr, kb).astype(jnp.float32) * scale
    s_mk = jnp.einsum('bnqkgd,bskd->bnkgqs', qr, k_meta).astype(jnp.float32) * scale
    blk = jnp.arange(nb)[:, None, None]
    qi = jnp.arange(BLOCK)[None, :, None]
    sj = jnp.arange(3 * BLOCK)[None, None, :]
    key_pos = (blk - 1) * BLOCK + sj
    q_pos = blk * BLOCK + qi
    visible = (jnp.abs(key_pos - q_pos) <= WINDOW) & (key_pos >= 0) & (key_pos < nb * BLOCK)
    s_band = jnp.where(visible[None, :, None, None], s_band, NEG_INF)
    p = sink_softmax(jnp.concatenate([s_mk, s_band], axis=-1)).astype(v.dtype)
    o_real = (jnp.einsum('bnkgqs,bnskd->bnqkgd', p[..., N_META:], vb)
              + jnp.einsum('bnkgqs,bskd->bnqkgd', p[..., :N_META], v_meta))
    o_real = o_real.reshape(B, nb * BLOCK, GQA_KV_HEADS, GQA_GROUP, GQA_HEAD_DIM)
    o = jnp.concatenate([o_meta, o_real], axis=1)
    return o.reshape(B, L, GQA_HEADS * GQA_HEAD_DIM)


def encoder_layer(h, lam_init, attn_norm_g, w_in, da_q_norm_g, da_k_norm_g, da_lam_q1, da_lam_k1,
                  da_lam_q2, da_lam_k2, da_subln_g, mla_cq_norm_g, mla_ckv_norm_g, mla_w_uq, mla_w_ukv,
                  mla_q_norm_g, mla_k_norm_g, gqa_q_norm_g, gqa_k_norm_g, gqa_sink, w_branch, w_out,
                  mlp_norm_g, w_up, w_down):
    B, L, _ = h.shape
    xp = rms_norm(h, attn_norm_g) @ w_in
    (da_q, da_k, da_v, mla_cq, mla_ckv, mla_kr,
     gqa_q, gqa_k, gqa_v, gate_logits) = jnp.split(xp, IN_SPLITS, axis=-1)
    o_da = diff_attention(da_q, da_k, da_v, da_q_norm_g, da_k_norm_g, da_lam_q1, da_lam_k1,
                          da_lam_q2, da_lam_k2, da_subln_g, lam_init)
    o_mla = mla_attention(mla_cq, mla_ckv, mla_kr, mla_cq_norm_g, mla_ckv_norm_g, mla_w_uq, mla_w_ukv,
                          mla_q_norm_g, mla_k_norm_g)
    o_gqa = window_gqa_attention(gqa_q, gqa_k, gqa_v, gqa_q_norm_g, gqa_k_norm_g, gqa_sink)
    branches = jnp.stack([o_da, o_mla, o_gqa], axis=2)
    proj = jnp.einsum('blgc,gcd->blgd', branches, w_branch)
    gates = jax.nn.sigmoid(gate_logits.astype(jnp.float32)).reshape(B, L, N_BRANCH, D_MODEL)
    merged = jnp.sum(gates * proj.astype(jnp.float32), axis=2).astype(h.dtype)
    h = h + merged @ w_out
    u = rms_norm(h, mlp_norm_g) @ w_up
    h = h + jnp.square(jax.nn.relu(u)) @ w_down
    return h


def encoder_trunk(x, meta_tokens, layer_weights):
    B = x.shape[0]
    meta = jnp.broadcast_to(meta_tokens[None].astype(x.dtype), (B, N_META, D_MODEL))
    h = jnp.concatenate([meta, x], axis=1)
    for l in range(DEPTH):
        lam_init = 0.8 - 0.6 * math.exp(-0.3 * l)
        h = encoder_layer(h, lam_init, *[w[l] for w in layer_weights])
    return h[:, N_META:]


def setup_inputs(seed: int = 0) -> dict:
    key = jax.random.key(seed)
    ks = jax.random.split(key, 32)
    f32 = jnp.float32

    def nrm(k, shape, scale):
        return jax.random.normal(k, shape, f32) * scale

    def gain(k, shape):
        return 1.0 + 0.02 * jax.random.normal(k, shape, f32)

    return {
        "x_prompt": nrm(ks[0], (BATCH, SEQ, D_MODEL), 1.0),
        "x_sample": nrm(ks[1], (DEC_BATCH, DEC_SEQ, D_MODEL), 1.0),
        "meta_tokens": nrm(ks[2], (N_META, D_MODEL), 1.0),
        "attn_norm_g": gain(ks[3], (DEPTH, D_MODEL)),
        "w_in": nrm(ks[4], (DEPTH, D_MODEL, IN_WIDTH), D_MODEL ** -0.5),
        "da_q_norm_g": gain(ks[5], (DEPTH, DA_HEAD_DIM)),
        "da_k_norm_g": gain(ks[6], (DEPTH, DA_HEAD_DIM)),
        "da_lam_q1": nrm(ks[7], (DEPTH, DA_HEAD_DIM), 0.1),
        "da_lam_k1": nrm(ks[8], (DEPTH, DA_HEAD_DIM), 0.1),
        "da_lam_q2": nrm(ks[9], (DEPTH, DA_HEAD_DIM), 0.1),
        "da_lam_k2": nrm(ks[10], (DEPTH, DA_HEAD_DIM), 0.1),
        "da_subln_g": gain(ks[11], (DEPTH, DA_V_DIM)),
        "mla_cq_norm_g": gain(ks[12], (DEPTH, MLA_Q_RANK)),
        "mla_ckv_norm_g": gain(ks[13], (DEPTH, MLA_KV_RANK)),
        "mla_w_uq": nrm(ks[14], (DEPTH, MLA_Q_RANK, MLA_HEADS * MLA_QK_DIM), MLA_Q_RANK ** -0.5),
        "mla_w_ukv": nrm(ks[15], (DEPTH, MLA_KV_RANK, MLA_HEADS * (MLA_NOPE + MLA_V)), MLA_KV_RANK ** -0.5),
        "mla_q_norm_g": gain(ks[16], (DEPTH, MLA_QK_DIM)),
        "mla_k_norm_g": gain(ks[17], (DEPTH, MLA_QK_DIM)),
        "gqa_q_norm_g": gain(ks[18], (DEPTH, GQA_HEAD_DIM)),
        "gqa_k_norm_g": gain(ks[19], (DEPTH, GQA_HEAD_DIM)),
        "gqa_sink": nrm(ks[20], (DEPTH, GQA_HEADS), 0.5),
        "w_branch": nrm(ks[21], (DEPTH, N_BRANCH, BRANCH_WIDTH, D_MODEL), BRANCH_WIDTH ** -0.5),
        "w_out": nrm(ks[22], (DEPTH, D_MODEL, D_MODEL), D_MODEL ** -0.5),
        "mlp_norm_g": gain(ks[23], (DEPTH, D_MODEL)),
        "w_up": nrm(ks[24], (DEPTH, D_MODEL, D_FF), D_MODEL ** -0.5),
        "w_down": nrm(ks[25], (DEPTH, D_FF, D_MODEL), D_FF ** -0.5),
    }


def reference(x_prompt, x_sample, meta_tokens, attn_norm_g, w_in, da_q_norm_g, da_k_norm_g,
              da_lam_q1, da_lam_k1, da_lam_q2, da_lam_k2, da_subln_g, mla_cq_norm_g, mla_ckv_norm_g,
              mla_w_uq, mla_w_ukv, mla_q_norm_g, mla_k_norm_g, gqa_q_norm_g, gqa_k_norm_g, gqa_sink,
              w_branch, w_out, mlp_norm_g, w_up, w_down):
    layer_weights = (attn_norm_g, w_in, da_q_norm_g, da_k_norm_g, da_lam_q1, da_lam_k1, da_lam_q2,
                     da_lam_k2, da_subln_g, mla_cq_norm_g, mla_ckv_norm_g, mla_w_uq, mla_w_ukv,
                     mla_q_norm_g, mla_k_norm_g, gqa_q_norm_g, gqa_k_norm_g, gqa_sink, w_branch, w_out,
                     mlp_norm_g, w_up, w_down)
    y_prompt = encoder_trunk(x_prompt, meta_tokens, layer_weights)
    y_sample = encoder_trunk(x_sample, meta_tokens, layer_weights)
    return (y_prompt, y_sample)
```

```python
import math
import numpy as np
import ml_dtypes
from contextlib import ExitStack
import concourse.bass as bass
import concourse.mybir as mybir
from concourse.bass_utils import run_bass_kernel_spmd

F32 = mybir.dt.float32
BF16 = mybir.dt.bfloat16
AF = mybir.ActivationFunctionType
ALU = mybir.AluOpType

D = 1024
NCORE = 8
NM = 16
DEPTH = 2
EPS = 1e-6
NU = 35
NUC = 9
NPV = 42
NCON = 128 * 8 + 16
UE = 4096
EPOCH = 3000


class Buf:
    __slots__ = ("w", "rs", "name")

    def __init__(self, name=""):
        self.w = None
        self.rs = []
        self.name = name


class Op:
    __slots__ = ("eng", "fn", "deps", "idx", "sig", "sigval", "snap", "dma", "chan", "chidx",
                 "waits", "order", "inc", "epoch")


class Prog:
    ENG = ["pe", "act", "dve", "pool", "sp"]

    def __init__(self):
        self.ops = {e: [] for e in self.ENG}
        self.all = []
        self.dry = False
        self.semcount = {}
        self.semlast = {}

    def add(self, eng, fn, reads=(), writes=(), semkey=None, inc=16):
        if self.dry:
            return None
        op = Op()
        op.eng = eng
        op.fn = fn
        op.sig = False
        op.dma = semkey is not None
        op.order = len(self.all)
        op.inc = inc
        deps = {}
        for b in reads:
            if b.w is not None:
                deps[b.w.order] = b.w
        for b in writes:
            if b.w is not None:
                deps[b.w.order] = b.w
            for r in b.rs:
                deps[r.order] = r
        if op.dma:
            prev = self.semlast.get(semkey)
            if prev is not None:
                deps[prev.order] = prev
            self.semlast[semkey] = op
            c = self.semcount.get(semkey, 0) + inc
            self.semcount[semkey] = c
            op.chan = ("sem", semkey)
            op.chidx = c
            op.sig = True
        else:
            op.chan = eng
            op.chidx = len(self.ops[eng])
        op.deps = [deps[k] for k in sorted(deps)]
        for b in reads:
            b.rs.append(op)
        for b in writes:
            b.w = op
            b.rs = []
        op.idx = len(self.ops[eng])
        self.ops[eng].append(op)
        self.all.append(op)
        return op

    def barrier(self):
        if self.dry:
            return
        allb = Buf("barrier")
        lasts = []
        for e in self.ENG:
            if self.ops[e]:
                lasts.append(self.ops[e][-1])
        for k, o in self.semlast.items():
            lasts.append(o)
        for e in self.ENG:
            op = self.add(e, None)
            dd = {o.order: o for o in lasts if o is not op}
            op.deps = [dd[k] for k in sorted(dd)]

    def resolve(self):
        seen = {e: {} for e in self.ENG}
        for op in self.all:
            E = op.eng
            s = seen[E]
            waits = []
            best = {}
            for d in op.deps:
                b = best.get(d.chan)
                if b is None or d.chidx > b.chidx:
                    best[d.chan] = d
            for d in sorted(best.values(), key=lambda x: -x.order):
                if (not d.dma) and d.eng == E and E in ("pe", "sp"):
                    continue
                if s.get(d.chan, -1) >= d.chidx:
                    continue
                waits.append(d)
                d.sig = True
                s[d.chan] = d.chidx
                for k, v in d.snap.items():
                    if s.get(k, -1) < v:
                        s[k] = v
            op.waits = waits
            op.snap = dict(s)
        self.nepoch = {}
        for e in self.ENG:
            c = 0
            ep = 0
            for op in self.ops[e]:
                if not op.dma:
                    if op.sig:
                        if c >= EPOCH:
                            c = 0
                            ep += 1
                        c += 1
                        op.sigval = c
                        op.epoch = ep
                else:
                    op.sigval = op.chidx
            self.nepoch[e] = ep + 1

    def emit(self, nc, es):
        self.resolve()
        esem = {e: [es.enter_context(nc.semaphore("e_%s%d" % (e, k))) for k in range(self.nepoch[e])] for e in self.ENG}
        dsem = {k: es.enter_context(nc.semaphore("d_%d" % i)) for i, k in enumerate(self.semcount)}
        block = es.enter_context(nc.Block())
        prog = self

        def run(e, name):
            for op in prog.ops[name]:
                ws = [(dsem[d.chan[1]] if d.dma else esem[d.eng][d.epoch], d.sigval) for d in op.waits]
                if op.fn is None:
                    for (sem, val) in ws:
                        e.wait_ge(sem, val)
                    if op.sig:
                        e.nop().then_inc(esem[name][op.epoch], 1)
                    continue
                for (sem, val) in ws[:-1]:
                    e.wait_ge(sem, val)
                ins = op.fn(e)
                if ws:
                    ins.wait_op(ws[-1][0], ws[-1][1], "sem-ge")
                if op.dma:
                    ins.then_inc(dsem[op.chan[1]], op.inc)
                elif op.sig:
                    ins.then_inc(esem[name][op.epoch], 1)

        @block.tensor
        def _(e):
            run(e, "pe")

        @block.scalar
        def _(e):
            run(e, "act")

        @block.vector
        def _(e):
            run(e, "dve")

        @block.gpsimd
        def _(e):
            run(e, "pool")

        @block.sync
        def _(e):
            run(e, "sp")


class Stream:
    def __init__(self, P, name, slots, bufs, depth):
        self.P = P
        self.name = name
        self.slots = slots
        self.bufs = bufs
        self.n = len(slots)
        self.depth = depth
        self.reqs = []
        self.i = 0
        self.issued = 0
        self.released = []
        self.fence_pos = []
        self.passed = 0

    def reset(self):
        self.i = 0
        self.issued = 0
        self.passed = 0
        self.released = [False] * len(self.reqs)

    def fence(self):
        if self.P.dry:
            self.fence_pos.append(len(self.reqs))
        else:
            self.passed += 1

    def get(self, src_fn_key):
        if self.P.dry:
            self.reqs.append(src_fn_key)
            i = len(self.reqs) - 1
            return self.slots[i % self.n], self.bufs[i % self.n], i
        i = self.i
        self.i += 1
        limf = self.fence_pos[self.passed] if self.passed < len(self.fence_pos) else len(self.reqs)
        lim = min(limf, i + self.depth + 1)
        while self.issued < lim:
            j = self.issued
            if j >= self.n:
                assert self.released[j - self.n], (self.name, "slot reuse before release", j, i)
            ld = self.reqs[j]
            slot = self.slots[j % self.n]
            self.P.add("sp", ld(slot), reads=[], writes=[self.bufs[j % self.n]],
                       semkey="%s%d" % (self.name, j % self.n))
            self.issued += 1
        return self.slots[i % self.n], self.bufs[i % self.n], i

    def release(self, i):
        if not self.P.dry:
            self.released[i] = True


class Rot:
    def __init__(self, items):
        self.items = items
        self.i = 0

    def next(self):
        x = self.items[self.i % len(self.items)]
        self.i += 1
        return x


def _unitize(W):
    K, N = W.shape
    KC = K // 128
    a = W.reshape(KC, 128, N).transpose(1, 0, 2).reshape(128, KC * N)
    out = np.zeros((128, UE), np.float32)
    out[:, :KC * N] = a
    return out


def _layer_units(inp, l):
    w_in = inp["w_in"][l]
    us = [w_in[:, 0:512], w_in[:, 512:1024], w_in[:, 1024:1536],
          np.concatenate([w_in[:, 1536:1920], w_in[:, 2176:2240], w_in[:, 2176:2240]], 1),
          np.concatenate([w_in[:, 1920:2176], w_in[:, 2752:2880], w_in[:, 2880:3008]], 1)]
    gq = w_in[:, 2240:2752].reshape(D, 8, 64)
    us.append(np.concatenate([np.concatenate([gq[:, c], gq[:, 4 + c]], 1) for c in range(4)], 1))
    uq = inp["mla_w_uq"][l].reshape(384, 4, 192)
    us.append(np.concatenate([uq[:, :, :128].reshape(384, 512), uq[:, :, 128:].reshape(384, 256)], 1))
    ukv = inp["mla_w_ukv"][l].reshape(256, 4, 256)
    us.append(np.concatenate([ukv[:, :, :128].reshape(256, 512), ukv[:, :, 128:].reshape(256, 512)], 1))
    for j in range(6):
        us.append(w_in[:, 3008 + 512 * j: 3008 + 512 * (j + 1)])
    for g in range(3):
        us.append(inp["w_branch"][l, g])
    for j in range(2):
        us.append(inp["w_out"][l][:, 512 * j:512 * (j + 1)])
    for j in range(8):
        us.append(inp["w_up"][l][:, 512 * j:512 * (j + 1)])
    for j in range(8):
        us.append(inp["w_down"][l][:, 128 * j:128 * (j + 1)])
    assert len(us) == NU
    return [_unitize(np.ascontiguousarray(u, dtype=np.float32)) for u in us]


def _pvec(inp, l):
    pv = np.zeros((128, NPV), np.float32)
    pv[:, 0:8] = inp["attn_norm_g"][l].reshape(8, 128).T
    pv[:, 8:16] = inp["mlp_norm_g"][l].reshape(8, 128).T
    pv[:, 16] = np.tile(inp["da_q_norm_g"][l], 2)
    pv[:, 17] = np.tile(inp["da_k_norm_g"][l], 2)
    pv[:, 18] = inp["da_subln_g"][l]
    pv[:, 19:22] = inp["mla_cq_norm_g"][l].reshape(3, 128).T
    pv[:, 22:24] = inp["mla_ckv_norm_g"][l].reshape(2, 128).T
    pv[:, 24] = inp["mla_q_norm_g"][l][:128]
    pv[:, 25] = np.tile(inp["mla_q_norm_g"][l][128:], 2)
    pv[:, 26] = inp["mla_k_norm_g"][l][:128]
    pv[:, 27] = np.tile(inp["mla_k_norm_g"][l][128:], 2)
    pv[:, 28] = np.tile(inp["gqa_q_norm_g"][l], 2)
    pv[:, 29] = np.tile(inp["gqa_k_norm_g"][l], 2)
    pv[:64, 30] = inp["da_lam_q1"][l]
    pv[:64, 31] = inp["da_lam_k1"][l]
    pv[:64, 32] = inp["da_lam_q2"][l]
    pv[:64, 33] = inp["da_lam_k2"][l]
    pv[:, 34:42] = inp["gqa_sink"][l][None, :]
    return pv


def _consts():
    c = np.zeros((128, NCON), np.float32)
    i = np.arange(128)
    c[:, 0:128] = np.eye(128)
    c[:, 128:256] = 1.0
    c[:, 256:384] = (i[:, None] // 64 == i[None, :] // 64)
    c[:64, 384:512] = 1.0
    c[64:, 512:640] = 1.0
    partner = np.where((i % 64) < 32, i + 32, i - 32)
    c[partner, 640 + i] = 1.0
    c[:, 768:896] = (i[:, None] >= i[None, :])
    c[:, 896:1024] = (i[:, None] <= i[None, :])
    c[:, 1024:1040] = (i[:, None] <= np.arange(16)[None, :] + 112)
    return c


def _rope(pos):
    inv = 1.0 / (10000.0 ** (np.arange(0, 64, 2, dtype=np.float32) / 64.0))
    ang = pos.astype(np.float32)[None, :] * inv.astype(np.float32)[:, None]
    cos = np.cos(ang).astype(np.float32)
    sin = np.sin(ang).astype(np.float32)
    p = np.arange(128)
    out = np.zeros((128, 2, pos.shape[0]), np.float32)
    out[:, 0, :] = cos[p % 32]
    sg = np.where((p % 64) < 32, -1.0, 1.0).astype(np.float32)
    out[:, 1, :] = sin[p % 32] * sg[:, None]
    return out


def prep(inp, TR):
    units = []
    for l in range(DEPTH):
        units += _layer_units(inp, l)
    while len(units) < NUC * NCORE:
        units.append(np.zeros((128, UE), np.float32))
    pv = np.concatenate([_pvec(inp, l) for l in range(DEPTH)], 1)
    con = _consts()
    maps = []
    for c in range(NCORE):
        pos = np.concatenate([np.arange(16), np.arange(16), 16 + c * TR + np.arange(TR),
                              16 + (c % 2) * TR + np.arange(TR)])
        sel = np.zeros((128, 52), np.float32)

        def setsel(g, kind, r):
            sel[:, (g * 3 + kind) * 8 + r] = 1.0
        setsel(0, 0, 0)
        if c > 0:
            setsel(0, 1, c - 1)
            sel[:, 48] = 1.0
        if c < 7:
            setsel(0, 2, c + 1)
            sel[:, 49] = 1.0
        setsel(1, 0, 2 * (c // 2))
        if c % 2 == 1:
            setsel(1, 1, c - 1)
            sel[:, 50] = 1.0
        if c % 2 == 0:
            setsel(1, 2, c + 1)
            sel[:, 51] = 1.0
        maps.append({
            "xg0": np.ascontiguousarray(inp["x_prompt"][0, c * TR:(c + 1) * TR]),
            "xg1": np.ascontiguousarray(inp["x_sample"][c // 2, (c % 2) * TR:(c % 2 + 1) * TR]),
            "meta": np.ascontiguousarray(inp["meta_tokens"]),
            "wsrc": np.ascontiguousarray(np.concatenate(units[c * NUC:(c + 1) * NUC], 0)),
            "pvec": pv, "consts": con, "rope": _rope(pos), "sel": sel,
        })
    return maps


def build(TR):
    nc = bass.Bass("TRN2", target_bir_lowering=False)
    T = 32 + 2 * TR
    W = min(512, TR)
    NTL = TR // W
    NB = W // 128
    NBLK = TR // 128
    RANKS = [8, 2]

    def din(name, shape, dt=F32):
        return nc.dram_tensor(name, shape, dt, kind="ExternalInput")

    xg = [din("xg0", [TR, D]), din("xg1", [TR, D])]
    meta_d = din("meta", [NM, D])
    wsrc = din("wsrc", [NUC * 128, UE])
    pvec_d = din("pvec", [128, DEPTH * NPV])
    consts_d = din("consts", [128, NCON])
    rope_d = din("rope", [128, 2, T])
    sel_d = din("sel", [128, 52])
    yout = [nc.dram_tensor("y0", [TR, D], F32, kind="ExternalOutput"),
            nc.dram_tensor("y1", [TR, D], F32, kind="ExternalOutput")]
    wbf_in = nc.dram_tensor("wbf_in", [NUC * 128, UE], BF16)
    wbf = nc.dram_tensor("wbf", [NCORE * NUC * 128, UE], BF16)
    hT = nc.dram_tensor("hT", [128, 8, T], F32)
    xnT = nc.dram_tensor("xnT", [128, 8, T], BF16)
    QT = nc.dram_tensor("QT", [128, 14, T], BF16)
    kvin = [nc.dram_tensor("kvin%d" % l, [18 * 128, TR], BF16) for l in range(DEPTH)]
    kvout = [nc.dram_tensor("kvout%d" % l, [8 * 18 * 128, TR], BF16) for l in range(DEPTH)]
    kvinS = [[nc.dram_tensor("kvinS%d_%d" % (l, j), [256, TR], BF16) for j in range(9)] for l in range(DEPTH)]
    kvoutS = [[nc.dram_tensor("kvoutS%d_%d" % (l, j), [512, TR], BF16) for j in range(9)] for l in range(DEPTH)]

    def kv_in_ap(l, g, unit):
        if g == 0:
            return kvin[l][unit * 128:(unit + 1) * 128, :]
        return kvinS[l][unit // 2][(unit % 2) * 128:(unit % 2 + 1) * 128, :]

    def kv_out_ap(l, g, r, unit):
        if g == 0:
            row = (r * 18 + unit) * 128
            return kvout[l][row:row + 128, :]
        row = (r * 2 + unit % 2) * 128
        return kvoutS[l][unit // 2][row:row + 128, :]
    halo_in = [nc.dram_tensor("haloin%d" % l, [256, 512], BF16) for l in range(DEPTH)]
    halo_out = [nc.dram_tensor("haloout%d" % l, [NCORE * 256, 512], BF16) for l in range(DEPTH)]
    gk = [nc.dram_tensor("gk%d" % l, [128, 2, TR], BF16) for l in range(DEPTH)]
    gv = [nc.dram_tensor("gv%d" % l, [128, 2, TR], BF16) for l in range(DEPTH)]

    es = ExitStack()
    NWORDS = 33792
    AR = es.enter_context(nc.sbuf_tensor("arena", [128, NWORDS], F32))
    constf = es.enter_context(nc.sbuf_tensor("constf", [128, NCON], F32))
    constb = es.enter_context(nc.sbuf_tensor("constb", [128, NCON], BF16))
    pv = es.enter_context(nc.sbuf_tensor("pv", [128, DEPTH * NPV], F32))
    pv2 = es.enter_context(nc.sbuf_tensor("pv2", [128, 64], F32))
    selt = es.enter_context(nc.sbuf_tensor("selt", [128, 52], F32))
    wring_t = es.enter_context(nc.sbuf_tensor("wring", [128, 5 * UE], BF16))
    Kmeta = es.enter_context(nc.sbuf_tensor("Kmeta", [128, 11, 32], BF16))
    Vmeta = es.enter_context(nc.sbuf_tensor("Vmeta", [128, 2, 1152], BF16))
    PS = es.enter_context(nc.psum_tensor("ps", [128, 4096], F32))

    P = Prog()
    psb = [Buf("ps%d" % i) for i in range(8)]

    def bank(i, w=512):
        return PS[:, i * 512:i * 512 + w]

    ident = constf[:, 0:128]
    onesf = constf[:, 128:256]
    onesb = constb[:, 128:256]
    bdiag = constb[:, 256:384]
    sello = constb[:, 384:512]
    selhi = constb[:, 512:640]
    Rm = constb[:, 640:768]
    triprev = constb[:, 768:896]
    trinext = constb[:, 896:1024]
    mmask = constb[:, 1024:1040]

    class Arena:
        def __init__(self):
            self.off = 0

        def f32(self, n):
            a = AR[:, self.off:self.off + n]
            self.off += n
            assert self.off <= NWORDS, self.off
            return a

        def bf(self, n):
            n2 = (n + 1) // 2
            a = AR[:, self.off:self.off + n2].bitcast(BF16)
            self.off += n2
            assert self.off <= NWORDS, self.off
            return a

    def v3(ap, b):
        return ap.rearrange("p (a b) -> p a b", b=b)

    def mm(out, lhsT, rhs, start, stop, reads, writes):
        return P.add("pe", lambda e: e.matmul(out, lhsT=lhsT, rhs=rhs, start=start, stop=stop), reads, writes)

    def tr(out, in_, reads, writes):
        k = in_.shape[0]
        return P.add("pe", lambda e: e.transpose(out, in_, ident[0:k, 0:k]), reads, writes)

    def act(out, in_, func, reads, writes, scale=1.0, bias=0.0):
        return P.add("act", lambda e: e.activation(out, in_, func, bias=bias, scale=scale), reads, writes)

    def tt(eng, out, in0, in1, op, reads, writes):
        return P.add(eng, lambda e: e.tensor_tensor(out, in0, in1, op), reads, writes)

    def stt(eng, out, in0, scalar, in1, op0, op1, reads, writes):
        eng = "dve"
        return P.add(eng, lambda e: e.scalar_tensor_tensor(out, in0, scalar, in1, op0, op1), reads, writes)

    def ts(eng, out, in0, s1, s2, op0, op1, reads, writes):
        return P.add(eng, lambda e: e.tensor_scalar(out, in0, s1, s2, op0, op1), reads, writes)

    def cp(eng, out, in_, reads, writes):
        if eng == "act":
            return P.add("act", lambda e: e.copy(out, in_), reads, writes)
        return P.add(eng, lambda e: e.tensor_copy(out, in_), reads, writes)

    def dma(out, in_, reads, writes, key, eng="sp"):
        return P.add(eng, lambda e: e.dma_start(out=out, in_=in_), reads, writes, semkey=key)

    def recip(out, in_, reads, writes):
        return P.add("dve", lambda e: e.reciprocal(out, in_), reads, writes)

    def rstd_from(ps_ap, n, out_ap, lnv_ap, reads, writes_l, writes_o):
        act(lnv_ap, ps_ap, AF.Ln, reads, writes_l, scale=1.0 / n, bias=EPS)
        act(out_ap, lnv_ap, AF.Exp, writes_l, writes_o, scale=-0.5)

    wslots = [wring_t[:, i * UE:(i + 1) * UE] for i in range(5)]
    wbufs = [Buf("w%d" % i) for i in range(5)]
    WS = Stream(P, "w", wslots, wbufs, 2)

    def wget(l, u):
        gu = l * NU + u
        return WS.get(lambda slot: (lambda e: e.dma_start(out=slot, in_=wbf[gu * 128:(gu + 1) * 128, :])))

    st = {}

    def wview(slot, KC, N):
        return v3(slot[:, 0:KC * N], N)

    def phaseA(l, psA, psN, psV):
        o = l * NPV
        ar = Arena()
        h = v3(ar.f32(8 * W), W)
        sq = v3(ar.bf(8 * W), W)
        xn = v3(ar.bf(8 * W), W)
        rstd = ar.f32(W)
        lnv0 = ar.f32(W)
        cs = v3(ar.f32(2 * W), W)
        xtok = [ar.f32(1024), ar.f32(1024)]
        sqh = [ar.bf(W) for _ in range(2)]
        rsd = [ar.f32(W) for _ in range(2)]
        lnv = [ar.f32(W) for _ in range(2)]
        qn = [ar.f32(W) for _ in range(2)]
        qnb = [ar.bf(W) for _ in range(2)]
        t1 = [ar.f32(W) for _ in range(2)]
        t2 = [ar.f32(W) for _ in range(2)]
        stage = [ar.bf(W) for _ in range(4)]
        Vst = v3(ar.bf(4 * 1152), 1152)
        cqraw = v3(ar.f32(3 * W), W)
        cqn = v3(ar.bf(3 * W), W)
        sq3 = v3(ar.bf(3 * W), W)
        ckvraw = v3(ar.f32(2 * W), W)
        ckvn = v3(ar.bf(2 * W), W)
        kr2raw = ar.f32(W)
        sqkr = ar.bf(W)
        uqraw = v3(ar.f32(6 * W), W)
        squr = v3(ar.bf(2 * W), W)
        uknraw = v3(ar.f32(4 * W), W)
        ropein = v3(ar.f32(2 * W), W)

        hb, sqb, xnb, rstdb, lnv0b, csb = Buf("h"), Buf(), Buf(), Buf(), Buf(), Buf()
        xtb = [Buf(), Buf()]
        sqhb = [Buf(), Buf()]
        rsdb = [Buf(), Buf()]
        lnvb = [Buf(), Buf()]
        qnbuf = [Buf(), Buf()]
        qnbb = [Buf(), Buf()]
        t1b = [Buf(), Buf()]
        t2b = [Buf(), Buf()]
        stb = [Buf() for _ in range(4)]
        vstb, vmb, kmb = Buf(), Buf(), Buf()
        cqrb, cqnb, sq3b, ckvrb, ckvnb, kr2b, sqkrb, uqrb, squrb, uknrb, ropeinb = [Buf() for _ in range(11)]
        dsink = Buf("dramsink")
        rot2 = Rot([0, 1])
        rotst = Rot([0, 1, 2, 3])

        tiles = [("meta", 0, 0)] + [("real", g, i) for g in range(2) for i in range(NTL)]
        for (kind, g, i) in tiles:
            if kind == "meta":
                t0, Wt, blocks = 0, 32, [(0, 16), (16, 16)]
            else:
                t0, Wt, blocks = 32 + g * TR + i * W, W, [(b * 128, 128) for b in range(NB)]

            if l == 0:
                if kind == "meta":
                    dma(xtok[0][0:16, :], meta_d[:, :], [], [xtb[0]], "xt0")
                    bk = psV.next()
                    for kc in range(8):
                        tr(PS[:, bk * 512 + kc * 16: bk * 512 + (kc + 1) * 16], xtok[0][0:16, kc * 128:(kc + 1) * 128],
                           [xtb[0]], [psb[bk]])
                    cp("dve", h[:, :, 0:16], v3(PS[:, bk * 512: bk * 512 + 128], 16), [psb[bk]], [hb])
                    cp("dve", h[:, :, 16:32], v3(PS[:, bk * 512: bk * 512 + 128], 16), [psb[bk]], [hb])
                else:
                    for b in range(NB):
                        s = b % 2
                        r0 = i * W + b * 128
                        dma(xtok[s], xg[g][r0:r0 + 128, :], [], [xtb[s]], "xt%d" % s)
                        for half in range(2):
                            bk = psV.next()
                            for q in range(4):
                                kc = half * 4 + q
                                tr(PS[:, bk * 512 + q * 128: bk * 512 + (q + 1) * 128], xtok[s][:, kc * 128:(kc + 1) * 128],
                                   [xtb[s]], [psb[bk]])
                            cp("dve" if half == 0 else "act", h[:, half * 4:half * 4 + 4, b * 128:(b + 1) * 128],
                               v3(bank(bk), 128), [psb[bk]], [hb])
                dma(hT[:, :, t0:t0 + Wt], h[:, :, :Wt], [hb], [dsink], "hst")
            else:
                dma(h[:, :, :Wt], hT[:, :, t0:t0 + Wt], [], [hb], "hld")
            dma(cs[:, :, :Wt], rope_d[:, :, t0:t0 + Wt], [], [csb], "csld")

            act(sq[:, :, :Wt], h[:, :, :Wt], AF.Square, [hb], [sqb])
            bn = psN.next()
            for kc in range(8):
                mm(bank(bn, Wt), onesb, sq[:, kc, :Wt], kc == 0, kc == 7, [sqb], [psb[bn]])
            rstd_from(bank(bn, Wt), 1024.0, rstd[:, :Wt], lnv0[:, :Wt], [psb[bn]], [lnv0b], [rstdb])
            for kc in range(8):
                stt("dve" if kc % 2 == 0 else "pool", xn[:, kc, :Wt], h[:, kc, :Wt], pv[:, o + kc:o + kc + 1],
                    rstd[:, :Wt], ALU.mult, ALU.mult, [hb, rstdb], [xnb])
            dma(xnT[:, :, t0:t0 + Wt], xn[:, :, :Wt], [xnb], [dsink], "xnst")

            def proj(wv, KC, c0, M, src, srcb, bk):
                for kc in range(KC):
                    mm(bank(bk, Wt)[0:M, :], wv[:, kc, c0:c0 + M], src[:, kc, :Wt], kc == 0, kc == KC - 1,
                       [srcb, wcur[0]], [psb[bk]])

            def rope_out(src, srcb, dst, dstb):
                s = rot2.next()
                cp("act", qnb[s][:, :Wt], src, [srcb], [qnbb[s]])
                bn2 = psN.next()
                mm(bank(bn2, Wt), Rm, qnb[s][:, :Wt], True, True, [qnbb[s]], [psb[bn2]])
                tt("pool", t1[s][:, :Wt], src, cs[:, 0, :Wt], ALU.mult, [srcb, csb], [t1b[s]])
                tt("dve", t2[s][:, :Wt], bank(bn2, Wt), cs[:, 1, :Wt], ALU.mult, [psb[bn2], csb], [t2b[s]])
                tt("pool", dst, t1[s][:, :Wt], t2[s][:, :Wt], ALU.add, [t1b[s], t2b[s]], [dstb])

            def out_q(qc):
                s = rotst.next()

                def fin():
                    dma(QT[:, qc, t0:t0 + Wt], stage[s][:, :Wt], [stb[s]], [dsink], "qst%d" % s)
                return stage[s][:, :Wt], stb[s], fin

            def out_k(unit, kmidx):
                if kind == "meta":
                    return Kmeta[:, kmidx, 0:32], kmb, (lambda: None)
                s = rotst.next()

                def fin():
                    if unit == "gk":
                        dma(gk[l][:, g, i * W:(i + 1) * W], stage[s][:, :Wt], [stb[s]], [dsink], "kst%d" % s)
                    else:
                        dma(kv_in_ap(l, g, unit)[:, i * W:(i + 1) * W], stage[s][:, :Wt], [stb[s]],
                            [dsink], "kst%d" % s)
                return stage[s][:, :Wt], stb[s], fin

            def head64(bk, gcol, dst, dstb):
                s = rot2.next()
                act(sqh[s][:, :Wt], bank(bk, Wt), AF.Square, [psb[bk]], [sqhb[s]])
                bn2 = psN.next()
                mm(bank(bn2, Wt), bdiag, sqh[s][:, :Wt], True, True, [sqhb[s]], [psb[bn2]])
                rstd_from(bank(bn2, Wt), 64.0, rsd[s][:, :Wt], lnv[s][:, :Wt], [psb[bn2]], [lnvb[s]], [rsdb[s]])
                stt("dve", qn[s][:, :Wt], bank(bk, Wt), pv[:, gcol:gcol + 1], rsd[s][:, :Wt], ALU.mult, ALU.mult,
                    [psb[bk], rsdb[s]], [qnbuf[s]])
                rope_out(qn[s][:, :Wt], qnbuf[s], dst, dstb)

            def vproj(wv, KC, c0, N, src, srcb, vc0):
                for bi, (cb0, n) in enumerate(blocks):
                    bk = psV.next()
                    for kc in range(KC):
                        mm(bank(bk)[0:n, 0:N], src[:, kc, cb0:cb0 + n], wv[:, kc, c0:c0 + N], kc == 0, kc == KC - 1,
                           [srcb, wcur[0]], [psb[bk]])
                    if kind == "meta":
                        cp("act" if bi % 2 else "dve", Vmeta[0:n, bi, vc0:vc0 + N], bank(bk)[0:n, 0:N], [psb[bk]], [vmb])
                    else:
                        cp("act" if bi % 2 else "dve", Vst[0:n, bi, vc0:vc0 + N], bank(bk)[0:n, 0:N], [psb[bk]], [vstb])

            wcur = [None]

            slot, wb_, wi = wget(l, 0)
            wcur[0] = wb_
            wv = wview(slot, 8, 512)
            for c in range(4):
                bk = psA.next()
                proj(wv, 8, c * 128, 128, xn, xnb, bk)
                dst, dstb, fin = out_q(c)
                head64(bk, o + 16, dst, dstb)
                fin()
            WS.release(wi)
            slot, wb_, wi = wget(l, 1)
            wcur[0] = wb_
            wv = wview(slot, 8, 512)
            for c in range(4):
                bk = psA.next()
                proj(wv, 8, c * 128, 128, xn, xnb, bk)
                dst, dstb, fin = out_k(c, c)
                head64(bk, o + 17, dst, dstb)
                fin()
            WS.release(wi)
            slot, wb_, wi = wget(l, 2)
            wcur[0] = wb_
            wv = wview(slot, 8, 512)
            vproj(wv, 8, 0, 512, xn, xnb, 0)
            WS.release(wi)
            slot, wb_, wi = wget(l, 3)
            wcur[0] = wb_
            wv = wview(slot, 8, 512)
            for j in range(3):
                bk = psA.next()
                proj(wv, 8, j * 128, 128, xn, xnb, bk)
                cp("act", cqraw[:, j, :Wt], bank(bk, Wt), [psb[bk]], [cqrb])
            bk = psA.next()
            proj(wv, 8, 384, 128, xn, xnb, bk)
            cp("act", kr2raw[:, :Wt], bank(bk, Wt), [psb[bk]], [kr2b])
            WS.release(wi)
            act(sq3[:, :, :Wt], cqraw[:, :, :Wt], AF.Square, [cqrb], [sq3b])
            bn = psN.next()
            for j in range(3):
                mm(bank(bn, Wt), onesb, sq3[:, j, :Wt], j == 0, j == 2, [sq3b], [psb[bn]])
            s = rot2.next()
            rstd_from(bank(bn, Wt), 384.0, rsd[s][:, :Wt], lnv[s][:, :Wt], [psb[bn]], [lnvb[s]], [rsdb[s]])
            for j in range(3):
                stt("dve", cqn[:, j, :Wt], cqraw[:, j, :Wt], pv[:, o + 19 + j:o + 20 + j], rsd[s][:, :Wt], ALU.mult, ALU.mult,
                    [cqrb, rsdb[s]], [cqnb])
            act(sqkr[:, :Wt], kr2raw[:, :Wt], AF.Square, [kr2b], [sqkrb])
            slot, wb_, wi = wget(l, 4)
            wcur[0] = wb_
            wv = wview(slot, 8, 512)
            for j in range(2):
                bk = psA.next()
                proj(wv, 8, j * 128, 128, xn, xnb, bk)
                cp("act", ckvraw[:, j, :Wt], bank(bk, Wt), [psb[bk]], [ckvrb])
            bk = psA.next()
            proj(wv, 8, 256, 128, xn, xnb, bk)
            dst, dstb, fin = out_k("gk", 10)
            head64(bk, o + 29, dst, dstb)
            fin()
            vproj(wv, 8, 384, 128, xn, xnb, 1024)
            WS.release(wi)
            act(sq3[:, 0:2, :Wt], ckvraw[:, :, :Wt], AF.Square, [ckvrb], [sq3b])
            bn = psN.next()
            for j in range(2):
                mm(bank(bn, Wt), onesb, sq3[:, j, :Wt], j == 0, j == 1, [sq3b], [psb[bn]])
            s = rot2.next()
            rstd_from(bank(bn, Wt), 256.0, rsd[s][:, :Wt], lnv[s][:, :Wt], [psb[bn]], [lnvb[s]], [rsdb[s]])
            for j in range(2):
                stt("dve", ckvn[:, j, :Wt], ckvraw[:, j, :Wt], pv[:, o + 22 + j:o + 23 + j], rsd[s][:, :Wt], ALU.mult, ALU.mult,
                    [ckvrb, rsdb[s]], [ckvnb])
            slot, wb_, wi = wget(l, 5)
            wcur[0] = wb_
            wv = wview(slot, 8, 512)
            for c in range(4):
                bk = psA.next()
                proj(wv, 8, c * 128, 128, xn, xnb, bk)
                dst, dstb, fin = out_q(10 + c)
                head64(bk, o + 28, dst, dstb)
                fin()
            WS.release(wi)
            slot, wb_, wi = wget(l, 6)
            wcur[0] = wb_
            wv = wview(slot, 3, 768)
            for j in range(6):
                bk = psA.next()
                proj(wv, 3, j * 128, 128, cqn, cqnb, bk)
                cp("act" if j % 2 else "dve", uqraw[:, j, :Wt], bank(bk, Wt), [psb[bk]], [uqrb])
            WS.release(wi)
            act(squr[:, :, :Wt], uqraw[:, 4:6, :Wt], AF.Square, [uqrb], [squrb])
            for hh in range(4):
                s = rot2.next()
                act(sqh[s][:, :Wt], uqraw[:, hh, :Wt], AF.Square, [uqrb], [sqhb[s]])
                bn = psN.next()
                mm(bank(bn, Wt), onesb, sqh[s][:, :Wt], True, False, [sqhb[s]], [psb[bn]])
                mm(bank(bn, Wt), sello if hh % 2 == 0 else selhi, squr[:, hh // 2, :Wt], False, True, [squrb], [psb[bn]])
                rstd_from(bank(bn, Wt), 192.0, rsd[s][:, :Wt], lnv[s][:, :Wt], [psb[bn]], [lnvb[s]], [rsdb[s]])
                dst, dstb, fin = out_q(4 + hh)
                stt("dve", dst, uqraw[:, hh, :Wt], pv[:, o + 24:o + 25], rsd[s][:, :Wt], ALU.mult, ALU.mult,
                    [uqrb, rsdb[s]], [dstb])
                fin()
                r0 = (hh % 2) * 64
                stt("dve", ropein[r0:r0 + 64, hh // 2, :Wt], uqraw[r0:r0 + 64, 4 + hh // 2, :Wt], pv[r0:r0 + 64, o + 25:o + 26],
                    rsd[s][r0:r0 + 64, :Wt], ALU.mult, ALU.mult, [uqrb, rsdb[s]], [ropeinb])
                if hh % 2 == 1:
                    dst, dstb, fin = out_q(8 + hh // 2)
                    rope_out(ropein[:, hh // 2, :Wt], ropeinb, dst, dstb)
                    fin()
            slot, wb_, wi = wget(l, 7)
            wcur[0] = wb_
            wv = wview(slot, 2, 1024)
            for j in range(4):
                bk = psA.next()
                proj(wv, 2, j * 128, 128, ckvn, ckvnb, bk)
                cp("act" if j % 2 else "dve", uknraw[:, j, :Wt], bank(bk, Wt), [psb[bk]], [uknrb])
            vproj(wv, 2, 512, 512, ckvn, ckvnb, 512)
            WS.release(wi)
            for hh in range(4):
                s = rot2.next()
                act(sqh[s][:, :Wt], uknraw[:, hh, :Wt], AF.Square, [uknrb], [sqhb[s]])
                bn = psN.next()
                mm(bank(bn, Wt), onesb, sqh[s][:, :Wt], True, False, [sqhb[s]], [psb[bn]])
                mm(bank(bn, Wt), sello, sqkr[:, :Wt], False, True, [sqkrb], [psb[bn]])
                rstd_from(bank(bn, Wt), 192.0, rsd[s][:, :Wt], lnv[s][:, :Wt], [psb[bn]], [lnvb[s]], [rsdb[s]])
                dst, dstb, fin = out_k(8 + hh, 4 + hh)
                stt("dve", dst, uknraw[:, hh, :Wt], pv[:, o + 26:o + 27], rsd[s][:, :Wt], ALU.mult, ALU.mult,
                    [uknrb, rsdb[s]], [dstb])
                fin()
                r0 = (hh % 2) * 64
                stt("dve", ropein[r0:r0 + 64, hh // 2, :Wt], kr2raw[r0:r0 + 64, :Wt], pv[r0:r0 + 64, o + 27:o + 28],
                    rsd[s][r0:r0 + 64, :Wt], ALU.mult, ALU.mult, [kr2b, rsdb[s]], [ropeinb])
                if hh % 2 == 1:
                    dst, dstb, fin = out_k(12 + hh // 2, 8 + hh // 2)
                    rope_out(ropein[:, hh // 2, :Wt], ropeinb, dst, dstb)
                    fin()
            if kind == "real":
                for hh in range(4):
                    for (u0, c0) in ((4, 0), (14, 512)):
                        dma(v3(kv_in_ap(l, g, u0 + hh)[:, i * W:(i + 1) * W], 128),
                            Vst[:, 0:NB, c0 + hh * 128:c0 + (hh + 1) * 128], [vstb], [dsink], "vst%d" % (hh % 2))
                dma(v3(gv[l][:, g, i * W:(i + 1) * W], 128), Vst[:, 0:NB, 1024:1152], [vstb], [dsink], "vst0")

    def gathers(l):
        hb_ = Buf()
        for g in range(2):
            r = slice(g * 128, (g + 1) * 128)
            dma(halo_in[l][r, 0:128], gk[l][:, g, 0:128], [], [hb_], "hl0")
            dma(halo_in[l][r, 128:256], gv[l][:, g, 0:128], [], [hb_], "hl1")
            dma(halo_in[l][r, 256:384], gk[l][:, g, TR - 128:TR], [], [hb_], "hl0")
            dma(halo_in[l][r, 384:512], gv[l][:, g, TR - 128:TR], [], [hb_], "hl1")
        P.add("pool", lambda e: e.collective_compute("AllGather", ALU.bypass, replica_groups=[list(range(NCORE))],
                                                     ins=[halo_in[l].ap().opt()], outs=[halo_out[l].ap().opt()]),
              [hb_], [hb_], semkey="cch", inc=1)
        P.add("pool", lambda e: e.collective_compute("AllGather", ALU.bypass, replica_groups=[list(range(NCORE))],
                                                     ins=[kvin[l].ap().opt()], outs=[kvout[l].ap().opt()]),
              [], [hb_], semkey="ccp", inc=1)
        for j in range(9):
            def ccs(e, j=j):
                return e.collective_compute("AllGather", ALU.bypass, replica_groups=[[0, 1], [2, 3], [4, 5], [6, 7]],
                                            ins=[kvinS[l][j].ap().opt()], outs=[kvoutS[l][j].ap().opt()])
            P.add("pool", ccs, [], [hb_], semkey="ccs", inc=1)

    def phaseCD(l):
        o = l * NPV
        lastl = (l == DEPTH - 1)
        ar = Arena()
        oT = v3(ar.bf(12 * W), W)
        oTb = Buf("oT")
        Kg = v3(ar.bf((NBLK + 2) * 128), 128)
        Vaug = ar.bf((NBLK + 2) * 130).rearrange("p (a b c) -> p a b c", b=2, c=65)
        Kf = ar.bf(128)
        Vfaug = ar.bf(130).rearrange("p (b c) -> p b c", c=65)
        Vmaug = ar.bf(2 * 130).rearrange("p (a b c) -> p a b c", b=2, c=65)
        maskp = ar.bf(128)
        maskn = ar.bf(128)
        gqb = Buf("gqa")
        mark = ar.off
        Qt = v3(ar.bf(14 * W), W)
        Qb = Buf("Qt")
        kvslots = [ar.bf(TR) for _ in range(7)]
        kvbufs = [Buf("kv%d" % i) for i in range(7)]
        Pt = [v3(ar.bf(2 * W), W) for _ in range(3)]
        Pb = [Buf() for _ in range(3)]
        dn = ar.f32(W)
        accP = v3(ar.f32(2 * W), W)
        accPb = Buf("accP")
        bcs = [ar.f32(W), ar.f32(W)]
        t1 = ar.f32(W)
        t2 = ar.f32(W)
        ot = ar.f32(W)
        osq = ar.bf(W)
        lnv = ar.f32(W)
        rs = ar.f32(W)
        fb = Buf("fin")
        halo = v3(ar.bf(8 * 512), 512)
        hacc = v3(ar.f32(3 * 256), 256)
        Vtmp = v3(ar.bf(NBLK * 128), 128)
        Pg = [v3(ar.bf(4 * 128), 128) for _ in range(2)]
        Pgb = [Buf(), Buf()]
        Og = v3(ar.f32(512), 64)
        dg = v3(ar.f32(8), 1)
        rdg = v3(ar.f32(8), 1)
        ogb = Buf()
        ar.off = mark
        h = v3(ar.f32(8 * W), W)
        B1 = v3(ar.bf(8 * W), W)
        B2 = v3(ar.bf(8 * W), W)
        acc = v3(ar.f32(8 * W), W)
        sig = [ar.f32(W), ar.f32(W)]
        tmp = [ar.f32(W), ar.f32(W)]
        rstd = ar.f32(W)
        lnv0 = ar.f32(W)
        uT = v3(ar.bf(32 * W), W)
        rr = [ar.f32(W), ar.f32(W)]
        ytok = [ar.f32(1024), ar.f32(1024)]
        hb, B1b, B2b, accb, rstdb, lnv0b, uTb = [Buf() for _ in range(7)]
        sigb = [Buf(), Buf()]
        tmpb = [Buf(), Buf()]
        rrb = [Buf(), Buf()]
        ytb = [Buf(), Buf()]
        dsink = Buf()

        KS = Stream(P, "kv%d_" % l, kvslots, kvbufs, 3)
        if ("KS", l) in st:
            KS.reqs = st[("KS", l)].reqs
            KS.fence_pos = st[("KS", l)].fence_pos
            KS.reset()
        else:
            st[("KS", l)] = KS
        psS = Rot([0, 1, 2, 3])
        rot2 = Rot([0, 1])
        rot3 = Rot([0, 1, 2])

        def kvget(g, r, unit):
            src = kv_out_ap(l, g, r, unit)
            return KS.get(lambda slot: (lambda e: e.dma_start(out=slot, in_=src)))

        def gqa_setup(g):
            dma(Kg[:, 1:NBLK + 1, :], v3(gk[l][:, g, :], 128), [], [gqb], "gq0")
            dma(Vtmp[:, :, :], v3(gv[l][:, g, :], 128), [], [gqb], "gq1")
            hv = halo_out[l].ap().rearrange("(r q) c -> q r c", q=256)
            dma(halo[:, :, :], hv[g * 128:(g + 1) * 128, :, :], [], [gqb], "gq2")
            P.add("pool", lambda e: e.memset(Vaug[:, :, :, 64:65], 1.0), [], [gqb])
            P.add("pool", lambda e: e.memset(Vfaug[:, :, 64:65], 1.0), [], [gqb])
            P.add("pool", lambda e: e.memset(Vmaug[:, :, :, 64:65], 1.0), [], [gqb])
            for kv in range(2):
                cp("pool", Vaug[:, 1:NBLK + 1, kv, 0:64], Vtmp[:, :, kv * 64:(kv + 1) * 64], [gqb], [gqb])
                for gg in range(2):
                    cp("dve", Vmaug[0:16, gg, kv, 0:64], Vmeta[0:16, gg, 1024 + kv * 64:1024 + (kv + 1) * 64], [gqb], [gqb])
            for k in range(3):
                c0 = 256 if k == 1 else 0
                for r in range(8):
                    sc = selt[:, (g * 3 + k) * 8 + r:(g * 3 + k) * 8 + r + 1]
                    if r == 0:
                        ts("dve", hacc[:, k, :], halo[:, r, c0:c0 + 256], sc, None, ALU.mult, ALU.bypass, [gqb], [gqb])
                    else:
                        stt("dve", hacc[:, k, :], halo[:, r, c0:c0 + 256], sc, hacc[:, k, :], ALU.mult, ALU.add, [gqb], [gqb])
            cp("dve", Kf[:, :], hacc[:, 0, 0:128], [gqb], [gqb])
            cp("dve", Kg[:, 0, :], hacc[:, 1, 0:128], [gqb], [gqb])
            cp("dve", Kg[:, NBLK + 1, :], hacc[:, 2, 0:128], [gqb], [gqb])
            for kv in range(2):
                cp("dve", Vfaug[:, kv, 0:64], hacc[:, 0, 128 + kv * 64:128 + (kv + 1) * 64], [gqb], [gqb])
                cp("dve", Vaug[:, 0, kv, 0:64], hacc[:, 1, 128 + kv * 64:128 + (kv + 1) * 64], [gqb], [gqb])
                cp("dve", Vaug[:, NBLK + 1, kv, 0:64], hacc[:, 2, 128 + kv * 64:128 + (kv + 1) * 64], [gqb], [gqb])
            ts("dve", maskp[:, :], triprev, selt[:, 48 + 2 * g:49 + 2 * g], None, ALU.mult, ALU.bypass, [gqb], [gqb])
            ts("dve", maskn[:, :], trinext, selt[:, 49 + 2 * g:50 + 2 * g], None, ALU.mult, ALU.bypass, [gqb], [gqb])

        def dense_head(kind, hh, g, Wt):
            scale = 0.125 if kind == "da" else 192.0 ** -0.5
            r0 = (hh % 2) * 64

            def sub_da(Kap, Vap):
                return [dict(qk=[(Kap[j * 64:(j + 1) * 64, :], Qt[j * 64:(j + 1) * 64, hh, :Wt])], V=Vap, ob=4 + j)
                        for j in range(2)]

            def sub_mla(Kn, Kr, Vap):
                return dict(qk=[(Kn, Qt[:, 4 + hh, :Wt]), (Kr[r0:r0 + 64, :], Qt[r0:r0 + 64, 8 + hh // 2, :Wt])], V=Vap, ob=4)

            def steps():
                for r in range(RANKS[g]):
                    if kind == "da":
                        Kc, Kb, Ki = kvget(g, r, hh)
                        Vc, Vb, Vi = kvget(g, r, 4 + hh)
                        ks = [(Kc, Kb, Ki)]
                    else:
                        Kc, Kb, Ki = kvget(g, r, 8 + hh)
                        Kr, Krb, Kri = kvget(g, r, 12 + hh // 2)
                        Vc, Vb, Vi = kvget(g, r, 14 + hh)
                        ks = [(Kc, Kb, Ki), (Kr, Krb, Kri)]
                    Vv = v3(Vc, 128)
                    if kind == "da":
                        for kt in range(NBLK):
                            lastk = kt == NBLK - 1
                            yield dict(nk=128, subs=sub_da(Kc[:, kt * 128:(kt + 1) * 128], Vv[:, kt, :]), Kb=[Kb], Vb=[Vb],
                                       relK=[Ki] if lastk else [], relV=[Vi] if lastk else [])
                    else:
                        for kt in range(0, NBLK, 2):
                            lastk = kt == NBLK - 2
                            yield dict(nk=128, subs=[sub_mla(Kc[:, (kt + u) * 128:(kt + u + 1) * 128],
                                                             Kr[:, (kt + u) * 128:(kt + u + 1) * 128], Vv[:, kt + u, :])
                                                     for u in range(2)],
                                       Kb=[Kb, Krb], Vb=[Vb], relK=[Ki, Kri] if lastk else [], relV=[Vi] if lastk else [])
                if kind == "da":
                    yield dict(nk=16, subs=sub_da(Kmeta[:, hh, g * 16:(g + 1) * 16], Vmeta[0:16, g, hh * 128:(hh + 1) * 128]),
                               Kb=[], Vb=[], relK=[], relV=[])
                else:
                    yield dict(nk=16, subs=[sub_mla(Kmeta[:, 4 + hh, g * 16:(g + 1) * 16], Kmeta[:, 8 + hh // 2, g * 16:(g + 1) * 16],
                                                    Vmeta[0:16, g, 512 + hh * 128:512 + (hh + 1) * 128])],
                               Kb=[], Vb=[], relK=[], relV=[])

            nsteps = RANKS[g] * NBLK + 1 if kind == "da" else RANKS[g] * (NBLK // 2) + 1

            def emit_qk(s, idx):
                s0 = idx % 2
                nk = s["nk"]
                s["s0"] = s0
                for u, sub in enumerate(s["subs"]):
                    bq = s0 + 2 * u
                    n = len(sub["qk"])
                    for qi, (lhsT, rhs) in enumerate(sub["qk"]):
                        mm(bank(bq, Wt)[0:nk, :], lhsT, rhs, qi == 0, qi == n - 1, s["Kb"] + [Qb], [psb[bq]])
                for i_ in s["relK"]:
                    KS.release(i_)

            def emit_rest(s, idx):
                s0 = s["s0"]
                nk = s["nk"]
                nsub = len(s["subs"])
                p = rot3.next()
                src = v3(PS[:, s0 * 512:(s0 + 4) * 512], 1024)[0:nk, 0:nsub, :Wt]
                act(Pt[p][0:nk, 0:nsub, :Wt], src, AF.Exp, [psb[s0 + 2 * u] for u in range(nsub)], [Pb[p]], scale=scale)
                for u, sub in enumerate(s["subs"]):
                    if kind == "da":
                        first, last = idx == 0, idx == nsteps - 1
                    else:
                        first, last = (idx == 0 and u == 0), (idx == nsteps - 1 and u == nsub - 1)
                    mm(bank(sub["ob"], Wt), sub["V"], Pt[p][0:nk, u, :Wt], first, last, s["Vb"] + [Pb[p]], [psb[sub["ob"]]])
                mm(PS[0:1, 6 * 512:6 * 512 + Wt], onesb[0:nk, 0:1], Pt[p][0:nk, 0, :Wt], idx == 0, idx == nsteps - 1,
                   [Pb[p]], [psb[6]])
                if nsub == 2:
                    tt("dve", accP[0:nk, 1, :Wt], accP[0:nk, 1, :Wt], Pt[p][0:nk, 1, :Wt], ALU.add, [accPb, Pb[p]], [accPb])
                for i_ in s["relV"]:
                    KS.release(i_)

            P.add("pool", lambda e: e.memset(accP[:, 1, :Wt], 0.0), [], [accPb])
            it = steps()
            cur = next(it, None)
            idx = 0
            emit_qk(cur, 0)
            while cur is not None:
                nxt = next(it, None)
                if nxt is not None:
                    emit_qk(nxt, idx + 1)
                emit_rest(cur, idx)
                cur = nxt
                idx += 1
            assert idx == nsteps
            cp("dve", dn[0:1, :Wt], PS[0:1, 6 * 512:6 * 512 + Wt], [psb[6]], [fb])
            if kind == "mla":
                mm(bank(7, Wt), onesf, accP[:, 1, :Wt], True, False, [accPb], [psb[7]])
                mm(bank(7, Wt), onesf[0:1, :], dn[0:1, :Wt], False, True, [fb], [psb[7]])
                recip(bcs[0][:, :Wt], bank(7, Wt), [psb[7]], [fb])
                tt("dve", oT[:, 4 + hh, :Wt], bank(4, Wt), bcs[0][:, :Wt], ALU.mult, [psb[4], fb], [oTb])
                return
            mm(bank(7, Wt), onesf[0:1, :], dn[0:1, :Wt], True, True, [fb], [psb[7]])
            recip(bcs[0][:, :Wt], bank(7, Wt), [psb[7]], [fb])
            mm(bank(6, Wt), onesf, accP[:, 1, :Wt], True, True, [accPb], [psb[6]])
            recip(bcs[1][:, :Wt], bank(6, Wt), [psb[6]], [fb])
            tt("dve", t1[:, :Wt], bank(4, Wt), bcs[0][:, :Wt], ALU.mult, [psb[4], fb], [fb])
            tt("dve", t2[:, :Wt], bank(5, Wt), bcs[1][:, :Wt], ALU.mult, [psb[5], fb], [fb])
            stt("dve", ot[:, :Wt], t2[:, :Wt], pv2[:, 16 + l:17 + l], t1[:, :Wt], ALU.mult, ALU.add, [fb], [fb])
            act(osq[:, :Wt], ot[:, :Wt], AF.Square, [fb], [fb])
            mm(bank(7, Wt), onesb, osq[:, :Wt], True, True, [fb], [psb[7]])
            rstd_from(bank(7, Wt), 128.0, rs[:, :Wt], lnv[:, :Wt], [psb[7]], [fb], [fb])
            stt("dve", oT[:, hh, :Wt], ot[:, :Wt], pv2[:, 18 + l:19 + l], rs[:, :Wt], ALU.mult, ALU.mult, [fb], [oTb])

        def gqa_tile(kind, g, i, Wt):
            if kind == "meta":
                qblocks = [(0, 16, [(Kf, Vfaug, 128, mmask), (Kmeta[:, 10, g * 16:(g + 1) * 16], Vmaug[:, g], 16, None)])]
            else:
                qblocks = []
                for qb in range(NB):
                    gb = i * NB + qb
                    qblocks.append((qb * 128, 128, [
                        (Kg[:, gb, :], Vaug[:, gb], 128, maskp if gb == 0 else triprev),
                        (Kg[:, gb + 1, :], Vaug[:, gb + 1], 128, None),
                        (Kg[:, gb + 2, :], Vaug[:, gb + 2], 128, maskn if gb == NBLK - 1 else trinext),
                        (Kmeta[:, 10, g * 16:(g + 1) * 16], Vmaug[:, g], 16, None)]))
            for (q0, QW, ktl) in qblocks:
                for c in range(4):
                    for j in range(2):
                        head = c + 4 * j
                        rows = slice(j * 64, (j + 1) * 64)
                        sb = psS.next()
                        p = rot2.next()
                        for ti, (Kap, Vap, nk, mask) in enumerate(ktl):
                            mm(PS[0:nk, sb * 512 + ti * QW: sb * 512 + (ti + 1) * QW], Kap[rows, 0:nk],
                               Qt[rows, 10 + c, q0:q0 + QW], True, True, [gqb, Qb], [psb[sb]])
                        nfull = len(ktl) - 1
                        act(Pg[p][:, 0:nfull, :QW], v3(PS[:, sb * 512:sb * 512 + nfull * QW], QW), AF.Exp,
                            [psb[sb]], [Pgb[p]], scale=0.125)
                        act(Pg[p][0:16, nfull, :QW], PS[0:16, sb * 512 + nfull * QW: sb * 512 + (nfull + 1) * QW], AF.Exp,
                            [psb[sb]], [Pgb[p]], scale=0.125)
                        for ti, (Kap, Vap, nk, mask) in enumerate(ktl):
                            if mask is not None:
                                tt("pool", Pg[p][0:nk, ti, :QW], Pg[p][0:nk, ti, :QW], mask[0:nk, :QW], ALU.mult,
                                   [Pgb[p], gqb], [Pgb[p]])
                        ob = 4 + head // 4
                        oc = ob * 512 + (head % 4) * 65
                        for ti, (Kap, Vap, nk, mask) in enumerate(ktl):
                            mm(PS[0:QW, oc:oc + 65], Pg[p][0:nk, ti, :QW], Vap[0:nk, j, :], ti == 0, ti == len(ktl) - 1,
                               [Pgb[p], gqb], [psb[ob]])
                for half in range(2):
                    ov = v3(PS[0:QW, (4 + half) * 512:(4 + half) * 512 + 260], 65)
                    tt("dve", dg[0:QW, half * 4:half * 4 + 4, :], ov[:, :, 64:65],
                       v3(pv2[0:QW, 20 + 8 * l + half * 4:24 + 8 * l + half * 4], 1), ALU.add, [psb[4 + half]], [ogb])
                recip(rdg[0:QW, :, :], dg[0:QW, :, :], [ogb], [ogb])
                for head in range(8):
                    ov = v3(PS[0:QW, (4 + head // 4) * 512:(4 + head // 4) * 512 + 260], 65)
                    ts("dve", Og[0:QW, head, :], ov[:, head % 4, 0:64], rdg[0:QW, head, :], None, ALU.mult, ALU.bypass,
                       [psb[4 + head // 4], ogb], [ogb])
                Og2 = Og.rearrange("p a b -> p (a b)")
                for c in range(4):
                    tr(PS[:, 7 * 512 + c * QW:7 * 512 + (c + 1) * QW], Og2[0:QW, c * 128:(c + 1) * 128], [ogb], [psb[7]])
                cp("act", oT[:, 8:12, q0:q0 + QW], v3(PS[:, 7 * 512:7 * 512 + 4 * QW], QW), [psb[7]], [oTb])

        def phaseD(kind, g, i, t0, Wt):
            psA = Rot([0, 1, 2, 3])
            psG = Rot([4, 5])
            dma(h[:, :, :Wt], hT[:, :, t0:t0 + Wt], [], [hb], "dh")
            dma(B1[:, :, :Wt], xnT[:, :, t0:t0 + Wt], [], [B1b], "dx")
            for gi in range(3):
                ws_, wbb, wi = wget(l, 14 + gi)
                wvb = v3(ws_[:, 0:4096], 1024)
                wg = None
                for m in range(8):
                    if m % 4 == 0:
                        if wg is not None:
                            WS.release(wg[2])
                        wg = wget(l, 8 + gi * 2 + m // 4)
                        wvg = v3(wg[0][:, 0:4096], 512)
                    bg = psG.next()
                    for kc in range(8):
                        mm(bank(bg, Wt), wvg[:, kc, (m % 4) * 128:(m % 4 + 1) * 128], B1[:, kc, :Wt], kc == 0, kc == 7,
                           [wg[1], B1b], [psb[bg]])
                    bp = psA.next()
                    for kc in range(4):
                        mm(bank(bp, Wt), wvb[:, kc, m * 128:(m + 1) * 128], oT[:, gi * 4 + kc, :Wt], kc == 0, kc == 3,
                           [wbb, oTb], [psb[bp]])
                    s = rot2.next()
                    act(sig[s][:, :Wt], bank(bg, Wt), AF.Sigmoid, [psb[bg]], [sigb[s]])
                    if gi == 0:
                        tt("dve", acc[:, m, :Wt], bank(bp, Wt), sig[s][:, :Wt], ALU.mult, [psb[bp], sigb[s]], [accb])
                    else:
                        tt("dve", tmp[s][:, :Wt], bank(bp, Wt), sig[s][:, :Wt], ALU.mult, [psb[bp], sigb[s]], [tmpb[s]])
                        tt("pool", acc[:, m, :Wt], acc[:, m, :Wt], tmp[s][:, :Wt], ALU.add, [accb, tmpb[s]], [accb])
                WS.release(wg[2])
                WS.release(wi)
            cp("act", B2[:, :, :Wt], acc[:, :, :Wt], [accb], [B2b])
            for jn in range(2):
                ws_, wbb, wi = wget(l, 17 + jn)
                wv = v3(ws_[:, 0:4096], 512)
                for mq in range(4):
                    m = jn * 4 + mq
                    bp = psA.next()
                    for kc in range(8):
                        mm(bank(bp, Wt), wv[:, kc, mq * 128:(mq + 1) * 128], B2[:, kc, :Wt], kc == 0, kc == 7, [wbb, B2b], [psb[bp]])
                    tt("dve", h[:, m, :Wt], h[:, m, :Wt], bank(bp, Wt), ALU.add, [hb, psb[bp]], [hb])
                WS.release(wi)
            act(B2[:, :, :Wt], h[:, :, :Wt], AF.Square, [hb], [B2b])
            bn = psG.next()
            for kc in range(8):
                mm(bank(bn, Wt), onesb, B2[:, kc, :Wt], kc == 0, kc == 7, [B2b], [psb[bn]])
            rstd_from(bank(bn, Wt), 1024.0, rstd[:, :Wt], lnv0[:, :Wt], [psb[bn]], [lnv0b], [rstdb])
            for kc in range(8):
                stt("dve" if kc % 2 == 0 else "pool", B1[:, kc, :Wt], h[:, kc, :Wt], pv[:, o + 8 + kc:o + 9 + kc],
                    rstd[:, :Wt], ALU.mult, ALU.mult, [hb, rstdb], [B1b])
            for jn in range(8):
                ws_, wbb, wi = wget(l, 19 + jn)
                wv = v3(ws_[:, 0:4096], 512)
                for mq in range(4):
                    m = jn * 4 + mq
                    bp = psA.next()
                    for kc in range(8):
                        mm(bank(bp, Wt), wv[:, kc, mq * 128:(mq + 1) * 128], B1[:, kc, :Wt], kc == 0, kc == 7, [wbb, B1b], [psb[bp]])
                    s = rot2.next()
                    act(rr[s][:, :Wt], bank(bp, Wt), AF.Relu, [psb[bp]], [rrb[s]])
                    tt("dve" if m % 2 == 0 else "pool", uT[:, m, :Wt], rr[s][:, :Wt], rr[s][:, :Wt], ALU.mult, [rrb[s]], [uTb])
                WS.release(wi)
            for m in range(8):
                ws_, wbb, wi = wget(l, 27 + m)
                wv = v3(ws_[:, 0:4096], 128)
                bp = psA.next()
                for kc in range(32):
                    mm(bank(bp, Wt), wv[:, kc, :], uT[:, kc, :Wt], kc == 0, kc == 31, [wbb, uTb], [psb[bp]])
                tt("dve", h[:, m, :Wt], h[:, m, :Wt], bank(bp, Wt), ALU.add, [hb, psb[bp]], [hb])
                WS.release(wi)
            if not lastl:
                dma(hT[:, :, t0:t0 + Wt], h[:, :, :Wt], [hb], [dsink], "dhs")
            elif kind == "real":
                for b in range(NB):
                    s = b % 2
                    for half in range(2):
                        bk = psG.next()
                        for q in range(4):
                            kc = half * 4 + q
                            tr(PS[:, bk * 512 + q * 128:bk * 512 + (q + 1) * 128], h[:, kc, b * 128:(b + 1) * 128], [hb], [psb[bk]])
                        cp("dve" if half == 0 else "act", ytok[s][:, half * 512:(half + 1) * 512], bank(bk), [psb[bk]], [ytb[s]])
                    r0 = i * W + b * 128
                    dma(yout[g][r0:r0 + 128, :], ytok[s][:, :], [ytb[s]], [dsink], "yo%d" % s)

        for g in range(2):
            gqa_setup(g)
            tiles = [("meta", g, 0)] + [("real", g, i) for i in range(NTL)]
            for (kind, g_, i) in tiles:
                if kind == "meta":
                    t0, Wt = g * 16, 16
                else:
                    t0, Wt = 32 + g * TR + i * W, W
                dma(Qt[:, :, :Wt], QT[:, :, t0:t0 + Wt], [], [Qb], "ql")
                for hh in range(4):
                    dense_head("da", hh, g, Wt)
                for hh in range(4):
                    dense_head("mla", hh, g, Wt)
                gqa_tile(kind, g, i, Wt)
                KS.fence()
                P.barrier()
                if st["KSTOP"] == 4 and kind == "meta":
                    return
                if st["KSTOP"] == 5 and kind == "real":
                    return
                phaseD(kind, g, i, t0, Wt)
                P.barrier()
                if st["KSTOP"] == 6 and kind == "meta":
                    return
                if st["KSTOP"] == 7 and kind == "real":
                    return

    def body():
        psA = Rot([0, 1, 2, 3])
        psN = Rot([4, 5])
        psV = Rot([6, 7])
        cb = Buf("consts")
        dma(constf[:], consts_d[:, :], [], [cb], "c0")
        dma(pv[:], pvec_d[:, :], [], [cb], "c1")
        dma(selt[:], sel_d[:, :], [], [cb], "c2")
        cp("dve", constb[:], constf[:], [cb], [cb])
        P.add("dve", lambda e: e.memset(pv2[:], 0.0), [], [cb])
        P.barrier()
        for l in range(DEPTH):
            lam_init = 0.8 - 0.6 * math.exp(-0.3 * l)
            o = l * NPV
            tt("dve", pv2[:, 8:9], pv[:, o + 30:o + 31], pv[:, o + 31:o + 32], ALU.mult, [cb], [cb])
            tt("dve", pv2[:, 9:10], pv[:, o + 32:o + 33], pv[:, o + 33:o + 34], ALU.mult, [cb], [cb])
            mm(PS[:, 0:2], onesf, pv2[:, 8:10], True, True, [cb], [psb[0]])
            act(pv2[:, 10:12], PS[:, 0:2], AF.Exp, [psb[0]], [cb])
            tt("dve", pv2[:, 12:13], pv2[:, 11:12], pv2[:, 10:11], ALU.subtract, [cb], [cb])
            ts("dve", pv2[:, 16 + l:17 + l], pv2[:, 12:13], -lam_init, None, ALU.add, ALU.bypass, [cb], [cb])
            ts("dve", pv2[:, 18 + l:19 + l], pv[:, o + 18:o + 19], 1.0 - lam_init, None, ALU.mult, ALU.bypass, [cb], [cb])
            act(pv2[:, 20 + 8 * l:28 + 8 * l], pv[:, o + 34:o + 42], AF.Exp, [cb], [cb])
        P.barrier()

        ar = Arena()
        wst = [ar.f32(UE), ar.f32(UE)]
        wcv = [ar.bf(UE), ar.bf(UE)]
        wb = [Buf(), Buf()]
        wc = [Buf(), Buf()]
        wdst = Buf()
        for u in range(NUC):
            s = u % 2
            dma(wst[s], wsrc[u * 128:(u + 1) * 128, :], [], [wb[s]], "ws%d" % s)
            eng = ["dve", "pool", "act"][u % 3]
            cp(eng, wcv[s], wst[s], [wb[s]], [wc[s]])
            dma(wbf_in[u * 128:(u + 1) * 128, :], wcv[s], [wc[s]], [wdst], "wd%d" % s)
        P.add("pool", lambda e: e.collective_compute("AllGather", ALU.bypass, replica_groups=[list(range(NCORE))],
                                                     ins=[wbf_in.ap().opt()], outs=[wbf.ap().opt()]),
              [wdst], [wdst], semkey="ccw", inc=1)
        P.barrier()

        import os
        KSTOP = int(os.environ.get("KSTOP", "99"))
        st["KSTOP"] = KSTOP
        if KSTOP <= 1:
            return
        for l in range(DEPTH):
            phaseA(l, psA, psN, psV)
            P.barrier()
            if KSTOP <= 2:
                return
            gathers(l)
            P.barrier()
            if KSTOP <= 3:
                return
            phaseCD(l)
            P.barrier()


    P.dry = True
    body()
    P.dry = False
    WS.reset()
    body()
    P.emit(nc, es)
    es.close()
    return nc


_CACHE = {}


def kernel(**inputs):
    TR = inputs["x_prompt"].shape[1] // NCORE
    inputs = {k: np.asarray(v) for k, v in inputs.items()}
    if TR not in _CACHE:
        _CACHE[TR] = build(TR)
    nc = _CACHE[TR]
    maps = prep(inputs, TR)
    import os
    if os.environ.get("KTRACE"):
        res = run_bass_kernel_spmd(nc, maps, core_ids=list(range(NCORE)), trace=True)
        print("EXEC_TIME_NS", res.exec_time_ns)
    else:
        res = run_bass_kernel_spmd(nc, maps, core_ids=list(range(NCORE)))
    yp = np.concatenate([np.asarray(res.results[c]["y0"], dtype=np.float32) for c in range(NCORE)], 0)[None]
    ys = np.stack([np.concatenate([np.asarray(res.results[2 * b + j]["y1"], dtype=np.float32) for j in range(2)], 0)
                   for b in range(4)], 0)
    return (yp, ys)
```
